# Optimizing a Trainium2 kernel written in Bass

```python
import jax, jax.numpy as jnp
from jax import lax
import numpy as np

D_MODEL = 2048
BATCH = 4
SEQ = 4096
DEPTH = 1
DEC_BATCH = 2
DEC_SEQ = 8192
PAST_LEN = 128

D_MIX = D_MODEL
C_CONV = D_MIX // 2
CONV_WIDTH = 31
HEAD_DIM = 128
N_HEADS = (D_MIX - C_CONV) // HEAD_DIM
N_KV_HEADS = 2
GROUP = N_HEADS // N_KV_HEADS
WINDOW = 128
BLOCK = 128
ROPE_THETA = 10000.0
NORM_EPS = 1e-6
LN_EPS = 1e-5
D_ATTN = N_HEADS * HEAD_DIM
D_KV = N_KV_HEADS * HEAD_DIM
SPLITS = (C_CONV, C_CONV, C_CONV, D_ATTN, D_KV, D_KV, D_ATTN)
D_IN_PROJ = 3 * C_CONV + 2 * D_ATTN + 2 * D_KV

kernel_name = "hymba_conformer_swa_encoder"


def rms_norm(x, g):
    xf = x.astype(jnp.float32)
    y = xf * lax.rsqrt(jnp.mean(xf * xf, axis=-1, keepdims=True) + NORM_EPS)
    return (y * g.astype(jnp.float32)).astype(x.dtype)


def layer_norm(x, g, b):
    xf = x.astype(jnp.float32)
    mu = jnp.mean(xf, axis=-1, keepdims=True)
    xc = xf - mu
    var = jnp.mean(xc * xc, axis=-1, keepdims=True)
    y = xc * lax.rsqrt(var + LN_EPS) * g.astype(jnp.float32) + b.astype(jnp.float32)
    return y.astype(x.dtype)


def apply_rope(t, pos):
    half = HEAD_DIM // 2
    inv_freq = 1.0 / (ROPE_THETA ** (jnp.arange(half, dtype=jnp.float32) / half))
    ang = pos.astype(jnp.float32)[:, None] * inv_freq[None, :]
    cos = jnp.cos(ang)[None, :, None, :]
    sin = jnp.sin(ang)[None, :, None, :]
    tf = t.astype(jnp.float32)
    t1, t2 = tf[..., :half], tf[..., half:]
    out = jnp.concatenate([t1 * cos - t2 * sin, t2 * cos + t1 * sin], axis=-1)
    return out.astype(t.dtype)


def conformer_conv(u_val, u_glu, w_dw, b_dw, ln_g, ln_b, w_pw):
    h = u_val * jax.nn.sigmoid(u_glu)
    pad = CONV_WIDTH // 2
    h = lax.conv_general_dilated(
        h, w_dw[:, None, :].astype(h.dtype),
        window_strides=(1,), padding=[(pad, pad)],
        dimension_numbers=("NWC", "WIO", "NWC"),
        feature_group_count=C_CONV) + b_dw
    h = jax.nn.silu(layer_norm(h, ln_g, ln_b))
    return h @ w_pw


def banded_sink_attention(q, k, v, sink):
    B, S = q.shape[0], q.shape[1]
    nb = S // BLOCK
    qb = q.reshape(B, nb, BLOCK, N_KV_HEADS, GROUP, HEAD_DIM)

    def band(t):
        tp = jnp.pad(t, ((0, 0), (BLOCK, BLOCK), (0, 0), (0, 0)))
        tp = tp.reshape(B, nb + 2, BLOCK, N_KV_HEADS, HEAD_DIM)
        return jnp.concatenate([tp[:, :-2], tp[:, 1:-1], tp[:, 2:]], axis=2)

    kb, vb = band(k), band(v)
    scale = HEAD_DIM ** -0.5
    s = jnp.einsum("bnqkgd,bnjkd->bnkgqj", qb, kb).astype(jnp.float32) * scale
    blk = jnp.arange(nb)[:, None]
    qpos = blk * BLOCK + jnp.arange(BLOCK)[None, :]
    kpos = (blk - 1) * BLOCK + jnp.arange(3 * BLOCK)[None, :]
    rel = kpos[:, None, :] - qpos[:, :, None]
    valid = (jnp.abs(rel) <= WINDOW) & (kpos[:, None, :] >= 0) & (kpos[:, None, :] < S)
    s = jnp.where(valid[None, :, None, None], s, -1e30)
    sink_f = sink.astype(jnp.float32).reshape(1, 1, N_KV_HEADS, GROUP, 1, 1)
    m = jnp.maximum(jnp.max(s, axis=-1, keepdims=True), sink_f)
    p = jnp.exp(s - m)
    denom = jnp.sum(p, axis=-1, keepdims=True) + jnp.exp(sink_f - m)
    probs = (p / denom).astype(v.dtype)
    o = jnp.einsum("bnkgqj,bnjkd->bnqkgd", probs, vb)
    return o.reshape(B, S, D_ATTN)


def mixer_layer(x, norm_g, w_in, w_dw, b_dw, ln_g, ln_b, w_pw, sink, w_out):
    B, S, _ = x.shape
    h = rms_norm(x, norm_g)
    z = h @ w_in
    split_points = [int(i) for i in np.cumsum(SPLITS)[:-1]]
    c_val, c_glu, c_gate, q, k, v, a_gate = jnp.split(z, split_points, axis=-1)
    pos = jnp.arange(S)
    q = apply_rope(q.reshape(B, S, N_HEADS, HEAD_DIM), pos)
    k = apply_rope(k.reshape(B, S, N_KV_HEADS, HEAD_DIM), pos)
    v = v.reshape(B, S, N_KV_HEADS, HEAD_DIM)
    conv_out = conformer_conv(c_val, c_glu, w_dw, b_dw, ln_g, ln_b, w_pw) * jax.nn.silu(c_gate)
    attn_out = banded_sink_attention(q, k, v, sink) * jax.nn.silu(a_gate)
    mix = jnp.concatenate([conv_out, attn_out], axis=-1) @ w_out
    return x + mix


def encoder(x, norm_g, w_in, w_dw, b_dw, conv_ln_g, conv_ln_b, w_pw, attn_sink, w_out, final_norm_g):
    for l in range(DEPTH):
        x = mixer_layer(x, norm_g[l], w_in[l], w_dw[l], b_dw[l], conv_ln_g[l], conv_ln_b[l],
                        w_pw[l], attn_sink[l], w_out[l])
    return rms_norm(x, final_norm_g)


def setup_inputs(seed: int = 0) -> dict:
    key = jax.random.key(seed)
    ks = jax.random.split(key, 14)
    f32 = jnp.float32
    return {
        "x_prompt": jax.random.normal(ks[0], (BATCH, SEQ, D_MODEL), f32),
        "x_sample": jax.random.normal(ks[1], (DEC_BATCH, DEC_SEQ, D_MODEL), f32),
        "norm_g": 1.0 + 0.02 * jax.random.normal(ks[2], (DEPTH, D_MODEL), f32),
        "w_in": jax.random.normal(ks[3], (DEPTH, D_MODEL, D_IN_PROJ), f32) * D_MODEL ** -0.5,
        "w_dw": jax.random.normal(ks[4], (DEPTH, CONV_WIDTH, C_CONV), f32) * CONV_WIDTH ** -0.5,
        "b_dw": 0.02 * jax.random.normal(ks[5], (DEPTH, C_CONV), f32),
        "conv_ln_g": 1.0 + 0.02 * jax.random.normal(ks[6], (DEPTH, C_CONV), f32),
        "conv_ln_b": 0.02 * jax.random.normal(ks[7], (DEPTH, C_CONV), f32),
        "w_pw": jax.random.normal(ks[8], (DEPTH, C_CONV, C_CONV), f32) * C_CONV ** -0.5,
        "attn_sink": jax.random.normal(ks[9], (DEPTH, N_HEADS), f32),
        "w_out": jax.random.normal(ks[10], (DEPTH, D_MIX, D_MODEL), f32) * D_MIX ** -0.5,
        "final_norm_g": 1.0 + 0.02 * jax.random.normal(ks[11], (D_MODEL,), f32),
    }


def reference(x_prompt, x_sample, norm_g, w_in, w_dw, b_dw, conv_ln_g, conv_ln_b, w_pw,
              attn_sink, w_out, final_norm_g):
    y_prompt = encoder(x_prompt, norm_g, w_in, w_dw, b_dw, conv_ln_g, conv_ln_b, w_pw,
                       attn_sink, w_out, final_norm_g)
    y_sample = encoder(x_sample, norm_g, w_in, w_dw, b_dw, conv_ln_g, conv_ln_b, w_pw,
                       attn_sink, w_out, final_norm_g)
    return (y_prompt, y_sample)
```

```python
import math
import os
from contextlib import ExitStack

import numpy as np
import concourse.bass as bass
import concourse.mybir as mybir
from concourse.bass_utils import run_bass_kernel_spmd

F32 = mybir.dt.float32
BF16 = mybir.dt.bfloat16
ALU = mybir.AluOpType
AF = mybir.ActivationFunctionType

NCORES = 8
D = 2048
OWN = 4096
NG = 8
STREAM = OWN + 256
NSLOT = 32
NV = 48 + 248
NPE = 20
NWS = 4
ND = 24
ENGS = ["pe", "act", "dve", "pool", "sp"]


class Sched:
    def __init__(self, nc, es):
        self.nc = nc
        self.q = {e: [] for e in ENGS}
        self.prog = {e: es.enter_context(nc.semaphore(f"prog_{e}")) for e in ["pe", "act", "dve", "pool"]}
        self.cnt = {e: 0 for e in self.prog}
        self.dsem = [es.enter_context(nc.semaphore(f"dma_{i}")) for i in range(ND)]
        self.duse = [0] * ND
        self.dpool = {"pool": list(range(0, 8)), "sp": list(range(8, ND))}
        self.dnext = {"pool": 0, "sp": 0}
        self.waited = {}
        self.lastw = {}
        self.readers = {}

    def _deps(self, reads, writes):
        need = {}

        def add(t):
            k = (t[0], t[1])
            if t[2] > need.get(k, 0):
                need[k] = t[2]

        for k in reads:
            t = self.lastw.get(k)
            if t is not None:
                add(t)
        for k in writes:
            t = self.lastw.get(k)
            if t is not None:
                add(t)
            for kk, v in self.readers.get(k, {}).items():
                add((kk[0], kk[1], v))
        return need

    def _commit(self, tok, reads, writes):
        for k in writes:
            self.lastw[k] = tok
            self.readers[k] = {}
        for k in reads:
            if k in writes:
                continue
            d = self.readers.setdefault(k, {})
            kk = (tok[0], tok[1])
            if tok[2] > d.get(kk, 0):
                d[kk] = tok[2]

    def _emit_waits(self, eng, need):
        for (kind, s), v in need.items():
            if kind == "eng" and s == "pe" and eng == "pe":
                continue
            wk = (eng, kind, s)
            if self.waited.get(wk, 0) >= v:
                continue
            self.waited[wk] = v
            sem = self.prog[s] if kind == "eng" else self.dsem[s]
            self.q[eng].append(lambda e, sem=sem, v=v: e.wait_ge(sem, v))

    def op(self, eng, fn, reads=(), writes=()):
        reads = tuple(reads)
        writes = tuple(writes) + tuple(k for k in reads if k.startswith("bank") and k not in writes)
        self._emit_waits(eng, self._deps(reads, writes))
        self.cnt[eng] += 1
        tok = ("eng", eng, self.cnt[eng])
        sem = self.prog[eng]
        self.q[eng].append(lambda e, fn=fn, sem=sem: fn(e).then_inc(sem, 1))
        self._commit(tok, reads, writes)
        return tok

    def dma(self, eng, out, in_, reads=(), writes=()):
        reads = tuple(reads)
        writes = tuple(writes)
        lst = self.dpool[eng]
        s = lst[self.dnext[eng] % len(lst)]
        self.dnext[eng] += 1
        need = self._deps(reads, writes)
        if self.duse[s] > 0:
            need[("dma", s)] = max(need.get(("dma", s), 0), 16 * self.duse[s])
        self._emit_waits(eng, need)
        self.duse[s] += 1
        tok = ("dma", s, 16 * self.duse[s])
        sem = self.dsem[s]
        self.q[eng].append(lambda e, out=out, in_=in_, sem=sem: e.dma_start(out=out, in_=in_).then_inc(sem, 16))
        self._commit(tok, reads, writes)
        return tok

    def finish(self, eng):
        need = {}
        for e, c in self.cnt.items():
            if c:
                need[("eng", e)] = c
        for s in range(ND):
            if self.duse[s]:
                need[("dma", s)] = 16 * self.duse[s]
        self._emit_waits(eng, need)

    def emit(self, block):
        q = self.q

        @block.tensor
        def _(e):
            for f in q["pe"]:
                f(e)

        @block.scalar
        def _(e):
            for f in q["act"]:
                f(e)

        @block.vector
        def _(e):
            for f in q["dve"]:
                f(e)

        @block.gpsimd
        def _(e):
            for f in q["pool"]:
                f(e)

        @block.sync
        def _(e):
            for f in q["sp"]:
                f(e)


def build_nc(ngroups=NG, stage=99):
    nc = bass.Bass("TRN2", target_bir_lowering=False)
    xs = nc.dram_tensor("xs", [STREAM, D], F32, kind="ExternalInput").ap()
    wsl = nc.dram_tensor("wsl", [NSLOT, 128, 4096], F32, kind="ExternalInput").ap()
    cvec_d = nc.dram_tensor("cvec", [128, NV], F32, kind="ExternalInput").ap()
    gfin_d = nc.dram_tensor("gfin", [128, D], F32, kind="ExternalInput").ap()
    ident_d = nc.dram_tensor("ident", [128, 128], F32, kind="ExternalInput").ap()
    masks_d = nc.dram_tensor("masks", [128, 512], F32, kind="ExternalInput").ap()
    cs_d = nc.dram_tensor("cs", [2, 128, STREAM], F32, kind="ExternalInput").ap()
    y = nc.dram_tensor("y", [OWN, D], F32, kind="ExternalOutput").ap()
    wscr = nc.dram_tensor("wscr", [NSLOT, 128, 4096], BF16).ap()

    with ExitStack() as es:
        S = Sched(nc, es)
        sb = lambda name, shape, dt: es.enter_context(nc.sbuf_tensor(name, shape, dt))
        ps = lambda name, shape, dt: es.enter_context(nc.psum_tensor(name, shape, dt))

        xin = [sb(f"xin{i}", [128, D], F32) for i in range(2)]
        xn = [sb(f"xn{i}", [128, D], BF16) for i in range(2)]
        hT = sb("hT", [128, 16, 768], BF16)
        wsb = [sb(f"ws{i}", [128, 4096], BF16) for i in range(NWS)]
        u_h2ag = sb("u_h2ag", [128, 8 * 544], BF16)
        h2 = u_h2ag[:, :].rearrange("p (c t) -> p c t", c=8)
        kT = sb("kT", [128, 2, 768], BF16)
        vv = sb("vv", [128, 6, 256], BF16)
        qT = sb("qT", [128, 8, 512], BF16)
        ag2 = u_h2ag[:, 0:4096].rearrange("p (h t) -> p h t", h=8)
        mixT = sb("mixT", [128, 16, 512], BF16)
        r = sb("r", [128, 4, D], F32)
        hc = sb("hc", [128, 8, 512], BF16)
        sq = [sb(f"sq{i}", [128, 512], BF16) for i in range(2)]
        lnm = sb("lnm", [128, 512], F32)
        lnr = sb("lnr", [128, 512], F32)
        NT = 2
        tt = [sb(f"tt{i}", [128, 512], F32) for i in range(NT)]
        th = [sb(f"th{i}", [128, 512], F32) for i in range(2)]
        Pb = [sb(f"P{i}", [128, 3, 512], BF16) for i in range(2)]
        den = sb("den", [128, 512], F32)
        osb = sb("osb", [128, 512], F32)
        cost = sb("cost", [128, 768], F32)
        sint = sb("sint", [128, 768], F32)
        dgA = sb("dgA", [128, NPE, 128], BF16)
        acc = sb("acc", [128, 512], F32)
        gfin = sb("gfin_sb", [128, D], F32)
        identb = sb("identb", [128, 128], BF16)
        onesln = sb("onesln", [128, 128], BF16)
        ones1 = sb("ones1", [128, 128], BF16)
        negm = sb("negm", [128, 128], BF16)
        maskb = sb("maskb", [128, 4, 128], BF16)
        cv = sb("cv", [128, NV], F32)
        hl = sb("hl", [128, 16], F32)
        sinkexp = sb("sinkexp", [128, 8], F32)
        mhalf = sb("mhalf", [128, 1], F32)
        ssq = sb("ssq", [128, 8], F32)
        ssq2 = sb("ssq2", [128, 4], F32)
        rs = sb("rs", [128, 8], F32)

        lnv = osb
        rden = lnr
        identf = tt[1]
        maskf = tt[0]
        junk = hc
        banks = [ps(f"bank{i}", [128, 512], F32) for i in range(8)]
        if os.environ.get("KDEBUG"):
            print("sbuf bytes remaining", nc.sbuf_bytes_remaining)
        block = es.enter_context(nc.Block())

        st = {"bank": 0, "avail": list(range(8)), "tt": 0, "th": 0, "fe": 0, "tr": 0, "wl": 0, "wc": 0}

        def nbank():
            a = st["avail"]
            b = a[st["bank"] % len(a)]
            st["bank"] += 1
            return b

        def ntt():
            i = st["tt"] % NT
            st["tt"] += 1
            return i

        def nth():
            i = st["th"] % 2
            st["th"] += 1
            return i

        G_N, B_DW, LN_G, LN_B, SINK, W_DW = 0, 16, 24, 32, 40, 48

        S.dma("sp", cv[:, :], cvec_d, writes=["cv"])
        S.dma("sp", identf[:, 0:128], ident_d, writes=["tt1"])
        S.dma("sp", maskf[:, :], masks_d, writes=["tt0"])
        S.dma("sp", gfin[:, :], gfin_d, writes=["gfin"])
        S.op("dve", lambda e: e.tensor_copy(out=identb[:, :], in_=identf[:, 0:128]), reads=["tt1"], writes=["identb"])
        S.op("dve", lambda e: e.tensor_scalar(out=maskb[:, :, :], in0=maskf[:, :].rearrange("p (m t) -> p m t", m=4),
                                              scalar1=-1.0, scalar2=1.0, op0=ALU.mult, op1=ALU.add),
             reads=["tt0"], writes=["maskb"])
        S.op("dve", lambda e: e.tensor_scalar(out=negm[:, :], in0=identf[:, 0:128], scalar1=-30000.0, scalar2=None, op0=ALU.mult),
             reads=["tt1"], writes=["negm"])
        S.op("dve", lambda e: e.memset(onesln[:, :], 1.0 / 1024.0), writes=["onesln"])
        S.op("dve", lambda e: e.memset(ones1[:, :], 1.0), writes=["ones1"])
        S.op("pool", lambda e: e.memset(mhalf[:, :], -0.5), writes=["mhalf"])
        S.op("dve", lambda e: e.tensor_scalar(out=hl[:, :], in0=cv[:, LN_G:LN_G + 16], scalar1=0.5, scalar2=None, op0=ALU.mult),
             reads=["cv"], writes=["hl"])
        S.op("act", lambda e: e.activation(out=sinkexp[:, :], in_=cv[:, SINK:SINK + 8], func=AF.Exp), reads=["cv"], writes=["sinkexp"])

        order = list(range(NSLOT))
        total_loads = ngroups * NSLOT

        def issue_load(L):
            if L >= total_loads:
                return
            sl = order[L % NSLOT]
            if L < NSLOT:
                S.dma("pool", wsb[L % NWS][:, :], wsl[sl, :, :], writes=[f"ws{L % NWS}"])
                if total_loads > NSLOT:
                    S.dma("sp", wscr[sl, :, :], wsb[L % NWS][:, :], reads=[f"ws{L % NWS}"], writes=[f"scr{sl}"])
            else:
                S.dma("pool", wsb[L % NWS][:, :], wscr[sl, :, :], reads=[f"scr{sl}"], writes=[f"ws{L % NWS}"])

        def wslot():
            L = st["wc"]
            st["wc"] += 1
            while st["wl"] < min(L + NWS, total_loads):
                issue_load(st["wl"])
                st["wl"] += 1
            return wsb[L % NWS], f"ws{L % NWS}"

        def fe_a(g, b):
            xb = b % 2
            row = 512 * g + 128 * b
            S.dma("sp", xin[xb][:, :], xs[row:row + 128, :], writes=[f"xin{xb}"])
            S.op("act", lambda e: e.activation(out=xn[xb][:, :], in_=xin[xb][:, :], func=AF.Square, accum_out=ssq[:, b:b + 1]),
                 reads=[f"xin{xb}"], writes=[f"xn{xb}", f"ssq{b}"])
            S.op("dve", lambda e: e.tensor_scalar(out=ssq[:, b:b + 1], in0=ssq[:, b:b + 1], scalar1=1.0 / D, scalar2=1e-6,
                                                  op0=ALU.mult, op1=ALU.add), reads=[f"ssq{b}"], writes=[f"ssq{b}"])
            S.op("pool", lambda e: e.tensor_tensor(out=ssq[:, b:b + 1], in0=ssq[:, b:b + 1], in1=mhalf[:, 0:1], op=ALU.pow),
                 reads=[f"ssq{b}", "mhalf"], writes=[f"ssq{b}"])
            S.op("act", lambda e: e.activation(out=xn[xb][:, :], in_=xin[xb][:, :], func=AF.Copy, scale=ssq[:, b:b + 1]),
                 reads=[f"xin{xb}", f"ssq{b}"], writes=[f"xn{xb}"])

        def fe_b(g, b):
            xb = b % 2
            for u in range(4):
                bk = nbank()
                pb = banks[bk][:, :].bitcast(BF16)

                def tr(e, u=u, pb=pb, xb=xb):
                    ins = None
                    for j in range(4):
                        c = 4 * u + j
                        ins = e.transpose(out=pb[:, j * 128:(j + 1) * 128], in_=xn[xb][:, c * 128:(c + 1) * 128],
                                          identity=identb[:, :])
                    return ins
                S.op("pe", tr, reads=[f"xn{xb}", "identb"], writes=[f"bank{bk}"])
                S.op("dve", lambda e, u=u, pb=pb: e.tensor_tensor(
                    out=hT[:, 4 * u:4 * u + 4, b * 128:(b + 1) * 128],
                    in0=pb[:, 0:512].rearrange("p (j t) -> p j t", j=4),
                    in1=cv[:, G_N + 4 * u:G_N + 4 * u + 4].unsqueeze(2).to_broadcast([128, 4, 128]), op=ALU.mult),
                    reads=[f"bank{bk}", "cv"], writes=[f"hT{b}"])

        def fe_step(g, i):
            if i < 6:
                fe_a(g, i)
            if 1 <= i <= 6:
                fe_b(g, i - 1)

        HT_ALL = [f"hT{b}" for b in range(6)]
        AG_KEYS = [f"ag2_{h}" for h in range(8)]
        H2_KEYS = [f"h2_{c}" for c in range(8)]
        HT_OWN = [f"hT{b}" for b in range(1, 5)]

        def mm_unit(bk, ncols, wt, wkey, wcol0, tok0, nk=16, rhs_keys=HT_ALL):
            w3 = wt[:, :].rearrange("p (k c) -> p k c", k=16)

            def f(e):
                ins = None
                for k in range(nk):
                    ins = e.matmul(banks[bk][:, 0:ncols], lhsT=w3[:, k, wcol0:wcol0 + 128], rhs=hT[:, k, tok0:tok0 + ncols],
                                   start=(k == 0), stop=(k == nk - 1))
                return ins
            S.op("pe", f, reads=[wkey] + list(rhs_keys), writes=[f"bank{bk}"])

        def gate_evac(bk, ncols, out_ap, out_keys):
            ti = nth()
            S.op("act", lambda e: e.activation(out=th[ti][:, 0:ncols], in_=banks[bk][:, 0:ncols], func=AF.Tanh, scale=0.5),
                 reads=[f"bank{bk}"], writes=[f"th{ti}"])
            S.op("dve", lambda e: e.scalar_tensor_tensor(out=out_ap, in0=th[ti][:, 0:ncols], scalar=1.0, in1=banks[bk][:, 0:ncols],
                                                         op0=ALU.add, op1=ALU.mult),
                 reads=[f"th{ti}", f"bank{bk}"], writes=out_keys)

        def rope_evac(bk, ncols, pos0, out_ap, out_keys):
            ti = nth()
            S.op("act", lambda e: e.activation(out=th[ti][0:64, 0:ncols], in_=banks[bk][64:128, 0:ncols], func=AF.Copy),
                 reads=[f"bank{bk}"], writes=[f"th{ti}"])
            S.op("act", lambda e: e.activation(out=th[ti][64:128, 0:ncols], in_=banks[bk][0:64, 0:ncols], func=AF.Copy),
                 reads=[f"bank{bk}"], writes=[f"th{ti}"])
            t1 = ntt()
            S.op("dve", lambda e: e.tensor_tensor(out=tt[t1][:, 0:ncols], in0=banks[bk][:, 0:ncols], in1=cost[:, pos0:pos0 + ncols], op=ALU.mult),
                 reads=[f"bank{bk}", "cs"], writes=[f"tt{t1}"])
            S.op("dve", lambda e: e.tensor_tensor(out=th[ti][:, 0:ncols], in0=th[ti][:, 0:ncols], in1=sint[:, pos0:pos0 + ncols], op=ALU.mult),
                 reads=[f"th{ti}", "cs"], writes=[f"th{ti}"])
            S.op("dve", lambda e: e.tensor_tensor(out=out_ap, in0=tt[t1][:, 0:ncols], in1=th[ti][:, 0:ncols], op=ALU.add),
                 reads=[f"tt{t1}", f"th{ti}"], writes=out_keys)

        def group(g, last):
            S.dma("sp", cost[:, :], cs_d[0, :, 512 * g:512 * g + 768], writes=["cs"])
            S.dma("sp", sint[:, :], cs_d[1, :, 512 * g:512 * g + 768], writes=["cs"])

            if stage < 1:
                return
            st["avail"] = list(range(6))
            S1, S2 = 6, 7
            stats_fn = {}

            def conv_chunk(c):
                bk = nbank()
                half = NPE // 2
                for (t0_, t1_, key) in ((0, half, "dgA0"), (half, NPE, "dgA1")):
                    n_ = t1_ - t0_
                    S.op("pool", lambda e, t0_=t0_, n_=n_: e.tensor_tensor(
                        out=dgA[:, t0_:t0_ + n_, :], in0=identb[:, :].unsqueeze(1).to_broadcast([128, n_, 128]),
                        in1=cv[:, W_DW + 31 * c + t0_:W_DW + 31 * c + t0_ + n_].unsqueeze(2).to_broadcast([128, n_, 128]), op=ALU.mult),
                        reads=["identb", "cv"], writes=[key])

                    def fa(e, t0_=t0_, t1_=t1_):
                        ins = None
                        for tap in range(t0_, t1_):
                            ins = e.matmul(banks[bk][:, :], lhsT=dgA[:, tap, :], rhs=h2[:, c, tap + 1:tap + 513],
                                           start=(tap == 0), stop=(tap == NPE - 1))
                        return ins
                    S.op("pe", fa, reads=[key, f"h2_{c}"], writes=[f"bank{bk}"])
                accs = [(acc, "acc"), (tt[0], "tt0")]
                for n, tap in enumerate(range(NPE, 31)):
                    a_, ak = accs[n % 2]
                    wcol = cv[:, W_DW + 31 * c + tap:W_DW + 31 * c + tap + 1]
                    if n < 2:
                        S.op("dve", lambda e, a_=a_, tap=tap, wcol=wcol: e.tensor_scalar(
                            out=a_[:, :], in0=h2[:, c, tap + 1:tap + 513], scalar1=wcol, scalar2=None, op0=ALU.mult),
                            reads=[f"h2_{c}", "cv"], writes=[ak])
                    else:
                        S.op("dve", lambda e, a_=a_, tap=tap, wcol=wcol: e.scalar_tensor_tensor(
                            out=a_[:, :], in0=h2[:, c, tap + 1:tap + 513], scalar=wcol, in1=a_[:, :], op0=ALU.mult, op1=ALU.add),
                            reads=[f"h2_{c}", "cv", ak], writes=[ak])
                S.op("dve", lambda e: e.tensor_tensor(out=acc[:, :], in0=acc[:, :], in1=tt[0][:, :], op=ALU.add),
                     reads=["acc", "tt0"], writes=["acc"])
                S.op("dve", lambda e: e.tensor_tensor(out=tt[1][:, :], in0=banks[bk][:, :], in1=acc[:, :], op=ALU.add),
                     reads=[f"bank{bk}", "acc"], writes=["tt1"])
                si = c % 2
                S.op("act", lambda e: e.activation(out=hc[:, c, :], in_=tt[1][:, :], func=AF.Identity,
                                                   bias=cv[:, B_DW + c:B_DW + c + 1], scale=0.5),
                     reads=["tt1", "cv"], writes=[f"hc{c}"])
                S.op("act", lambda e: e.activation(out=sq[si][:, :], in_=tt[1][:, :], func=AF.Square,
                                                   bias=cv[:, B_DW + c:B_DW + c + 1], scale=0.5),
                     reads=["tt1", "cv"], writes=[f"sq{si}"])

                def fs(e):
                    e.matmul(banks[S1][:, :], lhsT=onesln[:, :], rhs=hc[:, c, :], start=(c == 0), stop=(c == 7))
                    return e.matmul(banks[S2][:, :], lhsT=onesln[:, :], rhs=sq[si][:, :], start=(c == 0), stop=(c == 7))
                stats_fn[c] = (fs, ["onesln", f"hc{c}", f"sq{si}"])
                if c > 0:
                    f_, rd_ = stats_fn[c - 1]
                    S.op("pe", f_, reads=rd_, writes=[f"bank{S1}", f"bank{S2}"])
                if c == 3:
                    for blk in range(4):
                        row = 512 * g + 128 + 128 * blk
                        S.dma("sp", r[:, blk, :], xs[row:row + 128, :], writes=[f"r{blk}"])

            for c in range(8):
                wt, wk = wslot()
                for hf in range(2):
                    tok0 = 112 + 272 * hf
                    bv = nbank()
                    mm_unit(bv, 272, wt, wk, 0, tok0)
                    bg = nbank()
                    mm_unit(bg, 272, wt, wk, 128, tok0)
                    ti = nth()
                    S.op("act", lambda e, bg=bg, ti=ti: e.activation(out=th[ti][:, 0:272], in_=banks[bg][:, 0:272], func=AF.Tanh, scale=0.5),
                         reads=[f"bank{bg}"], writes=[f"th{ti}"])
                    S.op("dve", lambda e, bv=bv, ti=ti, c=c, hf=hf: e.scalar_tensor_tensor(
                        out=h2[:, c, 272 * hf:272 * hf + 272], in0=th[ti][:, 0:272], scalar=1.0, in1=banks[bv][:, 0:272],
                        op0=ALU.add, op1=ALU.mult), reads=[f"th{ti}", f"bank{bv}", f"bank{bg}"], writes=[f"h2_{c}"] + AG_KEYS)
                if c > 0:
                    conv_chunk(c - 1)
            if stage < 1.3:
                return
            wt, wk = wslot()
            for kc in range(2):
                for hf in range(2):
                    bk = nbank()
                    mm_unit(bk, 384, wt, wk, 128 * kc, 384 * hf)
                    rope_evac(bk, 384, 384 * hf, kT[:, kc, 384 * hf:384 * hf + 384], [f"kT{kc}"])
            if stage < 1.6:
                return
            wt, wk = wslot()
            w3 = wt[:, :].rearrange("p (k c) -> p k c", k=16)
            for u in range(3):
                bk = nbank()

                def fv(e, u=u, bk=bk, w3=w3):
                    ins = None
                    for j in range(2):
                        b = 2 * u + j
                        for k in range(16):
                            ins = e.matmul(banks[bk][:, 256 * j:256 * j + 256], lhsT=hT[:, k, b * 128:(b + 1) * 128], rhs=w3[:, k, :],
                                           start=(k == 0), stop=(k == 15))
                    return ins
                S.op("pe", fv, reads=[wk] + HT_ALL, writes=[f"bank{bk}"])
                S.op("act", lambda e, u=u, bk=bk: e.activation(out=vv[:, 2 * u:2 * u + 2, :],
                                                               in_=banks[bk][:, :].rearrange("p (j c) -> p j c", j=2), func=AF.Copy),
                     reads=[f"bank{bk}"], writes=["vv"])

            conv_chunk(7)
            if stage < 2:
                return
            def ln_finalize():
              f_, rd_ = stats_fn[7]
              S.op("pe", f_, reads=rd_, writes=[f"bank{S1}", f"bank{S2}"])
              S.op("dve", lambda e: e.tensor_copy(out=lnm[:, :], in_=banks[S1][:, :]), reads=[f"bank{S1}"], writes=["lnm"])
              S.op("dve", lambda e: e.tensor_tensor(out=lnv[:, :], in0=banks[S1][:, :], in1=lnm[:, :], op=ALU.mult),
                   reads=[f"bank{S1}", "lnm"], writes=["osb", "onb0", "onb1"])
              S.op("dve", lambda e: e.tensor_tensor(out=lnv[:, :], in0=banks[S2][:, :], in1=lnv[:, :], op=ALU.subtract),
                   reads=[f"bank{S2}", "osb"], writes=["osb", "onb0", "onb1"])
              S.op("dve", lambda e: e.tensor_scalar(out=lnv[:, :], in0=lnv[:, :], scalar1=1e-5, scalar2=None, op0=ALU.add),
                   reads=["osb"], writes=["osb", "onb0", "onb1"])
              S.op("act", lambda e: e.activation(out=lnv[:, :], in_=lnv[:, :], func=AF.Ln), reads=["osb"], writes=["osb", "onb0", "onb1"])
              S.op("act", lambda e: e.activation(out=lnr[:, :], in_=lnv[:, :], func=AF.Exp, scale=-0.5), reads=["osb"], writes=["lnr"])
              S.op("dve", lambda e: e.scalar_tensor_tensor(out=lnm[:, :], in0=lnm[:, :], scalar=-1.0, in1=lnr[:, :], op0=ALU.mult, op1=ALU.mult),
                   reads=["lnm", "lnr"], writes=["lnm"])
            ln_tmp = {}
            ln_c = [0]

            def ln_s1(c):
                t1 = ntt()
                ti = nth()
                ln_tmp[c] = (t1, ti)
                S.op("dve", lambda e: e.tensor_tensor(out=tt[t1][:, :], in0=hc[:, c, :], in1=lnr[:, :], op=ALU.mult),
                     reads=[f"hc{c}", "lnr"], writes=[f"tt{t1}"])
                S.op("dve", lambda e: e.tensor_tensor(out=tt[t1][:, :], in0=tt[t1][:, :], in1=lnm[:, :], op=ALU.add),
                     reads=[f"tt{t1}", "lnm"], writes=[f"tt{t1}"])
                S.op("act", lambda e: e.activation(out=th[ti][:, :], in_=tt[t1][:, :], func=AF.Tanh,
                                                   bias=hl[:, 8 + c:9 + c], scale=hl[:, c:c + 1]),
                     reads=[f"tt{t1}", "hl"], writes=[f"th{ti}"])
                S.op("dve", lambda e: e.tensor_scalar(out=tt[t1][:, :], in0=tt[t1][:, :], scalar1=cv[:, LN_G + c:LN_G + c + 1],
                                                      scalar2=cv[:, LN_B + c:LN_B + c + 1], op0=ALU.mult, op1=ALU.add),
                     reads=[f"tt{t1}", "cv", f"th{ti}"], writes=[f"tt{t1}"])

            def ln_s2(c):
                t1, ti = ln_tmp[c]
                S.op("dve", lambda e: e.scalar_tensor_tensor(out=hc[:, c, :], in0=th[ti][:, :], scalar=1.0, in1=tt[t1][:, :],
                                                             op0=ALU.add, op1=ALU.mult),
                     reads=[f"th{ti}", f"tt{t1}"], writes=[f"hc{c}"])

            if stage < 3:
                return
            st["avail"] = list(range(6))
            HC_ALL = [f"hc{c}" for c in range(8)]
            pw_w = {}
            cg_w = {}

            def pw_unit(oc):
                wtp, wkp = pw_w[oc // 4]
                w8 = wtp[:, :].rearrange("p (k c) -> p k c", k=8)
                bk = nbank()

                def f(e):
                    ins = None
                    for k in range(8):
                        ins = e.matmul(banks[bk][:, :], lhsT=w8[:, k, (oc % 4) * 128:(oc % 4) * 128 + 128], rhs=hc[:, k, :],
                                       start=(k == 0), stop=(k == 7))
                    return ins
                S.op("pe", f, reads=[wkp] + HC_ALL, writes=[f"bank{bk}"])
                return bk

            for cgi in range(4):
                wt, wk = wslot()
                for j in range(2):
                    oc = 2 * cgi + j
                    bg = nbank()
                    mm_unit(bg, 512, wt, wk, 128 * j, 128, rhs_keys=HT_OWN)
                    S.op("act", lambda e, bg=bg: e.activation(out=den[:, :], in_=banks[bg][:, :], func=AF.Tanh, scale=0.5),
                         reads=[f"bank{bg}"], writes=["den"])
                    S.op("dve", lambda e, bg=bg, oc=oc: e.scalar_tensor_tensor(out=mixT[:, oc, :], in0=den[:, :], scalar=1.0, in1=banks[bg][:, :],
                                                                               op0=ALU.add, op1=ALU.mult),
                         reads=["den", f"bank{bg}"], writes=[f"mixT{oc}"])
                    if oc == 1:
                        ln_finalize()
                    if oc == 4:
                        st["avail"] = list(range(8))
                    if oc >= 2:
                        for _ in range(2):
                            if ln_c[0] < 8:
                                ln_s1(ln_c[0])
                                if ln_c[0] > 0:
                                    ln_s2(ln_c[0] - 1)
                                ln_c[0] += 1
            assert ln_c[0] == 8
            ln_s2(7)

            def q_head(wt, wk, j, h):
                bk = nbank()
                mm_unit(bk, 512, wt, wk, 128 * j, 128, rhs_keys=HT_OWN)
                rope_evac(bk, 512, 128, qT[:, h, :], [f"qT{h}"])

            def ag_head(wt, wk, j, h):
                bk = nbank()
                mm_unit(bk, 512, wt, wk, 128 * j, 128, rhs_keys=HT_OWN)
                gate_evac(bk, 512, ag2[:, h, :], [f"ag2_{h}"] + H2_KEYS)

            def pw_oc(oc):
                bp = pw_unit(oc)
                S.op("dve", lambda e, oc=oc, bp=bp: e.scalar_tensor_tensor(
                    out=mixT[:, oc, :], in0=banks[bp][:, :], scalar=0.25, in1=mixT[:, oc, :], op0=ALU.mult, op1=ALU.mult),
                    reads=[f"bank{bp}", f"mixT{oc}"], writes=[f"mixT{oc}"])

            if stage < 5:
                return
            scale = 1.0 / math.sqrt(128.0)
            nfe = 0
            units = [(qb, kvh) for kvh in range(2) for qb in range(4)]

            def att_scores(i):
                qb, kvh = units[i]
                Pi = Pb[i % 2]
                pk = f"P{i % 2}_"
                mprev = 2 if (g == 0 and qb == 0) else 0
                mnext = 3 if (last and qb == 3) else 1
                for j in range(3):
                    bk = nbank()
                    mi = {0: mprev, 1: None, 2: mnext}[j]

                    def fsc(e, j=j, qb=qb, kvh=kvh, bk=bk, mi=mi):
                        ins = e.matmul(banks[bk][:, :], lhsT=kT[:, kvh, (qb + j) * 128:(qb + j + 1) * 128],
                                       rhs=qT[:, 4 * kvh:4 * kvh + 4, qb * 128:(qb + 1) * 128], start=True, stop=(mi is None))
                        if mi is not None:
                            ins = e.matmul(banks[bk][:, :], lhsT=negm[:, :],
                                           rhs=maskb[:, mi, :].unsqueeze(1).to_broadcast([128, 4, 128]), start=False, stop=True)
                        return ins
                    S.op("pe", fsc, reads=[f"kT{kvh}", "negm", "maskb"] + [f"qT{4 * kvh + hh}" for hh in range(4)], writes=[f"bank{bk}"])
                    S.op("act", lambda e, j=j, bk=bk, Pi=Pi: e.activation(out=Pi[:, j, :], in_=banks[bk][:, :], func=AF.Exp, scale=scale),
                         reads=[f"bank{bk}"], writes=[pk + str(j)])

            onb = osb[:, :].bitcast(BF16)

            def att_rest_a(i):
                qb, kvh = units[i]
                Pi = Pb[i % 2]
                pkeys = [f"P{i % 2}_{j}" for j in range(3)]
                bo = nbank()

                def fo(e, bo=bo, qb=qb, kvh=kvh, Pi=Pi):
                    ins = None
                    for hh in range(4):
                        for j in range(3):
                            ins = e.matmul(banks[bo][:, hh * 128:(hh + 1) * 128], lhsT=Pi[:, j, hh * 128:(hh + 1) * 128],
                                           rhs=vv[:, qb + j, kvh * 128:(kvh + 1) * 128], start=(j == 0), stop=(j == 2))
                    return ins
                S.op("pe", fo, reads=["vv"] + pkeys, writes=[f"bank{bo}"])
                bd = nbank()

                def fden(e, bd=bd, Pi=Pi):
                    ins = None
                    for hh in range(4):
                        for j in range(3):
                            ins = e.matmul(banks[bd][:, hh:hh + 1], lhsT=Pi[:, j, hh * 128:(hh + 1) * 128], rhs=ones1[:, 0:1],
                                           start=(j == 0), stop=(j == 2))
                    return ins
                S.op("pe", fden, reads=["ones1"] + pkeys, writes=[f"bank{bd}"])
                S.op("dve", lambda e, bd=bd, kvh=kvh: e.tensor_tensor(out=rs[:, 0:4], in0=banks[bd][:, 0:4], in1=sinkexp[:, 4 * kvh:4 * kvh + 4], op=ALU.add),
                     reads=[f"bank{bd}", "sinkexp"], writes=["rs"])
                S.op("dve", lambda e: e.reciprocal(out=rs[:, 4:8], in_=rs[:, 0:4]), reads=["rs"], writes=["rr"])
                ob = i % 2
                S.op("dve", lambda e, bo=bo, ob=ob: e.tensor_tensor(
                    out=onb[:, 512 * ob:512 * ob + 512].rearrange("p (h d) -> p h d", h=4),
                    in0=banks[bo][:, :].rearrange("p (h d) -> p h d", h=4),
                    in1=rs[:, 4:8].unsqueeze(2).to_broadcast([128, 4, 128]), op=ALU.mult),
                    reads=[f"bank{bo}", "rr"], writes=[f"onb{ob}", "osb"])

            def att_rest_b(i):
                qb, kvh = units[i]
                ob = i % 2
                bt = nbank()
                pb = banks[bt][:, :].bitcast(BF16)

                def ft(e, pb=pb, ob=ob):
                    ins = None
                    for hh in range(4):
                        ins = e.transpose(out=pb[:, hh * 128:(hh + 1) * 128], in_=onb[:, 512 * ob + hh * 128:512 * ob + (hh + 1) * 128],
                                          identity=identb[:, :])
                    return ins
                S.op("pe", ft, reads=[f"onb{ob}", "identb"], writes=[f"bank{bt}"])
                S.op("dve", lambda e, qb=qb, kvh=kvh, pb=pb: e.scalar_tensor_tensor(
                    out=mixT[:, 8 + 4 * kvh:12 + 4 * kvh, qb * 128:(qb + 1) * 128],
                    in0=pb[:, 0:512].rearrange("p (h t) -> p h t", h=4), scalar=0.5,
                    in1=ag2[:, 4 * kvh:4 * kvh + 4, qb * 128:(qb + 1) * 128], op0=ALU.mult, op1=ALU.mult),
                    reads=[f"bank{bt}"] + [f"ag2_{4 * kvh + hh}" for hh in range(4)], writes=[f"mixT{8 + 4 * kvh + hh}" for hh in range(4)])

            def att_step(i):
                if i + 1 < 8:
                    att_scores(i + 1)
                att_rest_a(i)
                if i > 0:
                    att_rest_b(i - 1)
                if i == 7:
                    att_rest_b(7)

            for i in range(2):
                wt, wk = wslot()
                for j in range(2):
                    q_head(wt, wk, j, 2 * i + j)
            for i in range(2):
                wt, wk = wslot()
                for j in range(2):
                    ag_head(wt, wk, j, 2 * i + j)
            for i in range(2, 4):
                wt, wk = wslot()
                for j in range(2):
                    q_head(wt, wk, j, 2 * i + j)
                    n_ = 2 * (i - 2) + j
                    if n_ == 0:
                        att_scores(0)
                    else:
                        att_step(n_ - 1)
            for i in range(2, 4):
                wt, wk = wslot()
                for j in range(2):
                    ag_head(wt, wk, j, 2 * i + j)
                if i == 2:
                    att_step(3)
            for half in range(2):
                pw_w[half] = wslot()
                for jj in range(2):
                    pw_oc(4 * half + 2 * jj)
                    pw_oc(4 * half + 2 * jj + 1)
                    att_step(4 + 2 * half + jj)
                    if (not last) and half == 1:
                        fe_step(g + 1, nfe)
                        nfe += 1

            if stage < 6:
                return
            MIX_ALL = [f"mixT{k}" for k in range(16)]
            for cgi in range(8):
                wt, wk = wslot()
                w3o = wt[:, :].rearrange("p (k c) -> p k c", k=16)
                for pr in range(2):
                    bk = nbank()

                    def fw(e, pr=pr, bk=bk, w3o=w3o):
                        ins = None
                        for j in range(2):
                            blk = 2 * pr + j
                            for k in range(16):
                                ins = e.matmul(banks[bk][:, 256 * j:256 * j + 256], lhsT=mixT[:, k, blk * 128:(blk + 1) * 128], rhs=w3o[:, k, :],
                                               start=(k == 0), stop=(k == 15))
                        return ins
                    S.op("pe", fw, reads=[wk] + MIX_ALL, writes=[f"bank{bk}"])
                    S.op("dve", lambda e, pr=pr, bk=bk, cgi=cgi: e.tensor_tensor(
                        out=r[:, 2 * pr:2 * pr + 2, cgi * 256:(cgi + 1) * 256], in0=r[:, 2 * pr:2 * pr + 2, cgi * 256:(cgi + 1) * 256],
                        in1=banks[bk][:, :].rearrange("p (j c) -> p j c", j=2), op=ALU.add),
                        reads=[f"bank{bk}", f"r{2 * pr}", f"r{2 * pr + 1}"], writes=[f"r{2 * pr}", f"r{2 * pr + 1}"])
                    if (not last) and nfe < 7 and (2 * cgi + pr) % 2 == 0:
                        fe_step(g + 1, nfe)
                        nfe += 1
            for blk in range(4):
                S.op("act", lambda e, blk=blk: e.activation(out=junk[:, 0:4, :], in_=r[:, blk, :].rearrange("p (a b) -> p a b", a=4), func=AF.Square, accum_out=ssq2[:, blk:blk + 1]),
                     reads=[f"r{blk}"], writes=[f"hc{c_}" for c_ in range(4)] + [f"ssq2_{blk}"])
                S.op("dve", lambda e, blk=blk: e.tensor_scalar(out=ssq2[:, blk:blk + 1], in0=ssq2[:, blk:blk + 1], scalar1=1.0 / D, scalar2=1e-6,
                                                               op0=ALU.mult, op1=ALU.add), reads=[f"ssq2_{blk}"], writes=[f"ssq2_{blk}"])
                S.op("pool", lambda e, blk=blk: e.tensor_tensor(out=ssq2[:, blk:blk + 1], in0=ssq2[:, blk:blk + 1], in1=mhalf[:, 0:1], op=ALU.pow),
                     reads=[f"ssq2_{blk}", "mhalf"], writes=[f"ssq2_{blk}"])
                S.op("dve", lambda e, blk=blk: e.scalar_tensor_tensor(out=r[:, blk, :], in0=r[:, blk, :], scalar=ssq2[:, blk:blk + 1], in1=gfin[:, :],
                                                                      op0=ALU.mult, op1=ALU.mult),
                     reads=[f"r{blk}", f"ssq2_{blk}", "gfin"], writes=[f"r{blk}"])
                row = 512 * g + 128 * blk
                S.dma("sp", y[row:row + 128, :], r[:, blk, :], reads=[f"r{blk}"])

        for i in range(7):
            fe_step(0, i)
        for g in range(ngroups):
            group(g, g == ngroups - 1)
        S.finish("sp")
        S.emit(block)
    return nc


def _slot(m):
    K, C = m.shape
    nk = K // 128
    return np.ascontiguousarray(m.reshape(nk, 128, C).transpose(1, 0, 2)).reshape(128, nk * C)


def _build_slots(w_in, w_pw, w_out):
    sl = np.empty((NSLOT, 128, 4096), np.float32)
    V, GL, CG, Q, K, VV, AG = 0, 1024, 2048, 3072, 4096, 4352, 4608
    for c in range(8):
        sl[c] = _slot(np.concatenate([w_in[:, V + c * 128:V + (c + 1) * 128], w_in[:, GL + c * 128:GL + (c + 1) * 128]], axis=1))
    sl[8] = _slot(w_in[:, K:K + 256])
    sl[9] = _slot(w_in[:, VV:VV + 256])
    for i in range(4):
        sl[10 + i] = _slot(w_in[:, CG + 256 * i:CG + 256 * (i + 1)])
    for i in range(2):
        sl[14 + i] = _slot(w_in[:, Q + 256 * i:Q + 256 * (i + 1)])
        sl[16 + i] = _slot(w_in[:, AG + 256 * i:AG + 256 * (i + 1)])
        sl[18 + i] = _slot(w_in[:, Q + 256 * (i + 2):Q + 256 * (i + 3)])
        sl[20 + i] = _slot(w_in[:, AG + 256 * (i + 2):AG + 256 * (i + 3)])
    sl[22] = _slot(w_pw[:, 0:512])
    sl[23] = _slot(w_pw[:, 512:1024])
    for i in range(8):
        sl[24 + i] = _slot(w_out[:, 256 * i:256 * (i + 1)])
    return sl


def _rope_tables(pos):
    half = 64
    inv = (1.0 / (10000.0 ** (np.arange(half, dtype=np.float32) / half))).astype(np.float32)
    ang = pos.astype(np.float32)[None, :] * inv[:, None]
    c = np.cos(ang).astype(np.float32)
    s = np.sin(ang).astype(np.float32)
    cos = np.concatenate([c, c], axis=0)
    sin = np.concatenate([-s, s], axis=0)
    return np.stack([cos, sin], axis=0).astype(np.float32)


_NC_CACHE = {}


def kernel(x_prompt, x_sample, norm_g, w_in, w_dw, b_dw, conv_ln_g, conv_ln_b, w_pw, attn_sink, w_out, final_norm_g):
    x_prompt = np.asarray(x_prompt, np.float32)
    x_sample = np.asarray(x_sample, np.float32)
    w_in0 = np.asarray(w_in, np.float32)[0]
    w_pw0 = np.asarray(w_pw, np.float32)[0]
    w_out0 = np.asarray(w_out, np.float32)[0]
    slots = _build_slots(w_in0, w_pw0, w_out0)

    cvec = np.zeros((128, NV), np.float32)
    cvec[:, 0:16] = np.asarray(norm_g, np.float32)[0].reshape(16, 128).T
    cvec[:, 16:24] = np.asarray(b_dw, np.float32)[0].reshape(8, 128).T
    cvec[:, 24:32] = np.asarray(conv_ln_g, np.float32)[0].reshape(8, 128).T
    cvec[:, 32:40] = np.asarray(conv_ln_b, np.float32)[0].reshape(8, 128).T
    cvec[:, 40:48] = np.broadcast_to(np.asarray(attn_sink, np.float32)[0][None, :], (128, 8))
    wd = np.asarray(w_dw, np.float32)[0]
    cvec[:, 48:] = wd.reshape(31, 8, 128).transpose(2, 1, 0).reshape(128, 248)
    gfin = np.ascontiguousarray(np.broadcast_to(np.asarray(final_norm_g, np.float32)[None, :], (128, D)))
    ident = np.eye(128, dtype=np.float32)
    jj = np.arange(128)[:, None]
    qq = np.arange(128)[None, :]
    m_prev = (jj >= qq).astype(np.float32)
    m_next = (jj <= qq).astype(np.float32)

    in_maps = []
    for c in range(NCORES):
        if c < 4:
            seq = x_prompt[c]
            start, vl, vr = 0, False, False
        else:
            seq = x_sample[(c - 4) // 2]
            half = (c - 4) % 2
            start, vl, vr = half * OWN, half == 1, half == 0
        xs = np.zeros((STREAM, D), np.float32)
        xs[128:128 + OWN] = seq[start:start + OWN]
        if vl:
            xs[0:128] = seq[start - 128:start]
        if vr:
            xs[128 + OWN:] = seq[start + OWN:start + OWN + 128]
        masks = np.concatenate([m_prev, m_next, m_prev if vl else np.zeros_like(m_prev),
                                m_next if vr else np.zeros_like(m_next)], axis=1)
        pos = np.arange(STREAM, dtype=np.float32) + (start - 128)
        in_maps.append({"xs": xs, "wsl": slots, "cvec": cvec, "gfin": gfin, "ident": ident,
                        "masks": np.ascontiguousarray(masks), "cs": _rope_tables(pos)})

    if "nc" not in _NC_CACHE:
        _NC_CACHE["nc"] = build_nc()
    res = run_bass_kernel_spmd(_NC_CACHE["nc"], in_maps, core_ids=list(range(NCORES)))
    outs = [np.asarray(r_["y"], np.float32) for r_ in res.results]
    y_prompt = np.stack(outs[0:4], axis=0)
    y_sample = np.stack([np.concatenate([outs[4], outs[5]], axis=0), np.concatenate([outs[6], outs[7]], axis=0)], axis=0)
    return (y_prompt, y_sample)
```

```python
import math
import os
from contextlib import ExitStack

import numpy as np
import concourse.bass as bass
import concourse.mybir as mybir
from concourse.bass_utils import run_bass_kernel_spmd

F32 = mybir.dt.float32
BF16 = mybir.dt.bfloat16
ALU = mybir.AluOpType
AF = mybir.ActivationFunctionType

NCORES = 8
D = 2048
OWN = 4096
NG = 8
STREAM = OWN + 256
NSLOT = 32
NV = 48 + 248
NPE = 20
NWS = 4
ND = 24
ENGS = ["pe", "act", "dve", "pool", "sp"]


class Sched:
    def __init__(self, nc, es):
        self.nc = nc
        self.q = {e: [] for e in ENGS}
        self.prog = {e: es.enter_context(nc.semaphore(f"prog_{e}")) for e in ["pe", "act", "dve", "pool"]}
        self.cnt = {e: 0 for e in self.prog}
        self.dsem = [es.enter_context(nc.semaphore(f"dma_{i}")) for i in range(ND)]
        self.duse = [0] * ND
        self.dpool = {"pool": list(range(0, 8)), "sp": list(range(8, ND))}
        self.dnext = {"pool": 0, "sp": 0}
        self.waited = {}
        self.lastw = {}
        self.readers = {}

    def _deps(self, reads, writes):
        need = {}

        def add(t):
            k = (t[0], t[1])
            if t[2] > need.get(k, 0):
                need[k] = t[2]

        for k in reads:
            t = self.lastw.get(k)
            if t is not None:
                add(t)
        for k in writes:
            t = self.lastw.get(k)
            if t is not None:
                add(t)
            for kk, v in self.readers.get(k, {}).items():
                add((kk[0], kk[1], v))
        return need

    def _commit(self, tok, reads, writes):
        for k in writes:
            self.lastw[k] = tok
            self.readers[k] = {}
        for k in reads:
            if k in writes:
                continue
            d = self.readers.setdefault(k, {})
            kk = (tok[0], tok[1])
            if tok[2] > d.get(kk, 0):
                d[kk] = tok[2]

    def _emit_waits(self, eng, need):
        for (kind, s), v in need.items():
            if kind == "eng" and s == "pe" and eng == "pe":
                continue
            wk = (eng, kind, s)
            if self.waited.get(wk, 0) >= v:
                continue
            self.waited[wk] = v
            sem = self.prog[s] if kind == "eng" else self.dsem[s]
            self.q[eng].append(lambda e, sem=sem, v=v: e.wait_ge(sem, v))

    def op(self, eng, fn, reads=(), writes=()):
        reads = tuple(reads)
        writes = tuple(writes) + tuple(k for k in reads if k.startswith("bank") and k not in writes)
        self._emit_waits(eng, self._deps(reads, writes))
        self.cnt[eng] += 1
        tok = ("eng", eng, self.cnt[eng])
        sem = self.prog[eng]
        self.q[eng].append(lambda e, fn=fn, sem=sem: fn(e).then_inc(sem, 1))
        self._commit(tok, reads, writes)
        return tok

    def dma(self, eng, out, in_, reads=(), writes=()):
        reads = tuple(reads)
        writes = tuple(writes)
        lst = self.dpool[eng]
        s = lst[self.dnext[eng] % len(lst)]
        self.dnext[eng] += 1
        need = self._deps(reads, writes)
        if self.duse[s] > 0:
            need[("dma", s)] = max(need.get(("dma", s), 0), 16 * self.duse[s])
        self._emit_waits(eng, need)
        self.duse[s] += 1
        tok = ("dma", s, 16 * self.duse[s])
        sem = self.dsem[s]
        self.q[eng].append(lambda e, out=out, in_=in_, sem=sem: e.dma_start(out=out, in_=in_).then_inc(sem, 16))
        self._commit(tok, reads, writes)
        return tok

    def finish(self, eng):
        need = {}
        for e, c in self.cnt.items():
            if c:
                need[("eng", e)] = c
        for s in range(ND):
            if self.duse[s]:
                need[("dma", s)] = 16 * self.duse[s]
        self._emit_waits(eng, need)

    def emit(self, block):
        q = self.q

        @block.tensor
        def _(e):
            for f in q["pe"]:
                f(e)

        @block.scalar
        def _(e):
            for f in q["act"]:
                f(e)

        @block.vector
        def _(e):
            for f in q["dve"]:
                f(e)

        @block.gpsimd
        def _(e):
            for f in q["pool"]:
                f(e)

        @block.sync
        def _(e):
            for f in q["sp"]:
                f(e)


def build_nc(ngroups=NG, stage=99):
    nc = bass.Bass("TRN2", target_bir_lowering=False)
    xs = nc.dram_tensor("xs", [STREAM, D], F32, kind="ExternalInput").ap()
    wsl = nc.dram_tensor("wsl", [NSLOT, 128, 4096], F32, kind="ExternalInput").ap()
    cvec_d = nc.dram_tensor("cvec", [128, NV], F32, kind="ExternalInput").ap()
    gfin_d = nc.dram_tensor("gfin", [128, D], F32, kind="ExternalInput").ap()
    ident_d = nc.dram_tensor("ident", [128, 128], F32, kind="ExternalInput").ap()
    masks_d = nc.dram_tensor("masks", [128, 512], F32, kind="ExternalInput").ap()
    cs_d = nc.dram_tensor("cs", [2, 128, STREAM], F32, kind="ExternalInput").ap()
    y = nc.dram_tensor("y", [OWN, D], F32, kind="ExternalOutput").ap()
    wscr = nc.dram_tensor("wscr", [NSLOT, 128, 4096], BF16).ap()

    with ExitStack() as es:
        S = Sched(nc, es)
        sb = lambda name, shape, dt: es.enter_context(nc.sbuf_tensor(name, shape, dt))
        ps = lambda name, shape, dt: es.enter_context(nc.psum_tensor(name, shape, dt))

        xin = [sb(f"xin{i}", [128, D], F32) for i in range(2)]
        xn = [sb(f"xn{i}", [128, D], BF16) for i in range(2)]
        hT = sb("hT", [128, 16, 768], BF16)
        wsb = [sb(f"ws{i}", [128, 4096], BF16) for i in range(NWS)]
        u_h2ag = sb("u_h2ag", [128, 8 * 544], BF16)
        h2 = u_h2ag[:, :].rearrange("p (c t) -> p c t", c=8)
        kT = sb("kT", [128, 2, 768], BF16)
        vv = sb("vv", [128, 6, 256], BF16)
        qT = sb("qT", [128, 8, 512], BF16)
        ag2 = u_h2ag[:, 0:4096].rearrange("p (h t) -> p h t", h=8)
        mixT = sb("mixT", [128, 16, 512], BF16)
        r = sb("r", [128, 4, D], F32)
        hc = sb("hc", [128, 8, 512], BF16)
        sq = [sb(f"sq{i}", [128, 512], BF16) for i in range(2)]
        lnm = sb("lnm", [128, 512], F32)
        lnr = sb("lnr", [128, 512], F32)
        NT = 2
        tt = [sb(f"tt{i}", [128, 512], F32) for i in range(NT)]
        th = [sb(f"th{i}", [128, 512], F32) for i in range(2)]
        Pb = [sb(f"P{i}", [128, 3, 512], BF16) for i in range(2)]
        den = sb("den", [128, 512], F32)
        osb = sb("osb", [128, 512], F32)
        cost = sb("cost", [128, 768], F32)
        sint = sb("sint", [128, 768], F32)
        dgA = sb("dgA", [128, NPE, 128], BF16)
        acc = sb("acc", [128, 512], F32)
        gfin = sb("gfin_sb", [128, D], F32)
        identb = sb("identb", [128, 128], BF16)
        onesln = sb("onesln", [128, 128], BF16)
        ones1 = sb("ones1", [128, 128], BF16)
        negm = sb("negm", [128, 128], BF16)
        maskb = sb("maskb", [128, 4, 128], BF16)
        cv = sb("cv", [128, NV], F32)
        hl = sb("hl", [128, 16], F32)
        sinkexp = sb("sinkexp", [128, 8], F32)
        mhalf = sb("mhalf", [128, 1], F32)
        ssq = sb("ssq", [128, 8], F32)
        ssq2 = sb("ssq2", [128, 4], F32)
        rs = sb("rs", [128, 8], F32)

        lnv = osb
        rden = lnr
        identf = tt[1]
        maskf = tt[0]
        junk = hc
        banks = [ps(f"bank{i}", [128, 512], F32) for i in range(8)]
        if os.environ.get("KDEBUG"):
            print("sbuf bytes remaining", nc.sbuf_bytes_remaining)
        block = es.enter_context(nc.Block())

        st = {"bank": 0, "avail": list(range(8)), "tt": 0, "th": 0, "fe": 0, "tr": 0, "wl": 0, "wc": 0}

        def nbank():
            a = st["avail"]
            b = a[st["bank"] % len(a)]
            st["bank"] += 1
            return b

        def ntt():
            i = st["tt"] % NT
            st["tt"] += 1
            return i

        def nth():
            i = st["th"] % 2
            st["th"] += 1
            return i

        G_N, B_DW, LN_G, LN_B, SINK, W_DW = 0, 16, 24, 32, 40, 48

        S.dma("sp", cv[:, :], cvec_d, writes=["cv"])
        S.dma("sp", identf[:, 0:128], ident_d, writes=["tt1"])
        S.dma("sp", maskf[:, :], masks_d, writes=["tt0"])
        S.dma("sp", gfin[:, :], gfin_d, writes=["gfin"])
        S.op("dve", lambda e: e.tensor_copy(out=identb[:, :], in_=identf[:, 0:128]), reads=["tt1"], writes=["identb"])
        S.op("dve", lambda e: e.tensor_scalar(out=maskb[:, :, :], in0=maskf[:, :].rearrange("p (m t) -> p m t", m=4),
                                              scalar1=-1.0, scalar2=1.0, op0=ALU.mult, op1=ALU.add),
             reads=["tt0"], writes=["maskb"])
        S.op("dve", lambda e: e.tensor_scalar(out=negm[:, :], in0=identf[:, 0:128], scalar1=-30000.0, scalar2=None, op0=ALU.mult),
             reads=["tt1"], writes=["negm"])
        S.op("dve", lambda e: e.memset(onesln[:, :], 1.0 / 1024.0), writes=["onesln"])
        S.op("dve", lambda e: e.memset(ones1[:, :], 1.0), writes=["ones1"])
        S.op("pool", lambda e: e.memset(mhalf[:, :], -0.5), writes=["mhalf"])
        S.op("dve", lambda e: e.tensor_scalar(out=hl[:, :], in0=cv[:, LN_G:LN_G + 16], scalar1=0.5, scalar2=None, op0=ALU.mult),
             reads=["cv"], writes=["hl"])
        S.op("act", lambda e: e.activation(out=sinkexp[:, :], in_=cv[:, SINK:SINK + 8], func=AF.Exp), reads=["cv"], writes=["sinkexp"])

        order = list(range(NSLOT))
        total_loads = ngroups * NSLOT

        def issue_load(L):
            if L >= total_loads:
                return
            sl = order[L % NSLOT]
            if L < NSLOT:
                S.dma("pool", wsb[L % NWS][:, :], wsl[sl, :, :], writes=[f"ws{L % NWS}"])
                if total_loads > NSLOT:
                    S.dma("sp", wscr[sl, :, :], wsb[L % NWS][:, :], reads=[f"ws{L % NWS}"], writes=[f"scr{sl}"])
            else:
                S.dma("pool", wsb[L % NWS][:, :], wscr[sl, :, :], reads=[f"scr{sl}"], writes=[f"ws{L % NWS}"])

        def wslot():
            L = st["wc"]
            st["wc"] += 1
            while st["wl"] < min(L + NWS, total_loads):
                issue_load(st["wl"])
                st["wl"] += 1
            return wsb[L % NWS], f"ws{L % NWS}"

        def fe_a(g, b):
            xb = b % 2
            row = 512 * g + 128 * b
            if g == 0 and 1 <= b <= 4:
                src, skey = r[:, b - 1, :], f"r{b - 1}"
            else:
                src, skey = xin[xb][:, :], f"xin{xb}"
            S.dma("sp", src, xs[row:row + 128, :], writes=[skey])
            S.op("act", lambda e: e.activation(out=xn[xb][:, :], in_=src, func=AF.Square, accum_out=ssq[:, b:b + 1]),
                 reads=[skey], writes=[f"xn{xb}", f"ssq{b}"])
            S.op("dve", lambda e: e.tensor_scalar(out=ssq[:, b:b + 1], in0=ssq[:, b:b + 1], scalar1=1.0 / D, scalar2=1e-6,
                                                  op0=ALU.mult, op1=ALU.add), reads=[f"ssq{b}"], writes=[f"ssq{b}"])
            S.op("pool", lambda e: e.tensor_tensor(out=ssq[:, b:b + 1], in0=ssq[:, b:b + 1], in1=mhalf[:, 0:1], op=ALU.pow),
                 reads=[f"ssq{b}", "mhalf"], writes=[f"ssq{b}"])
            S.op("act", lambda e: e.activation(out=xn[xb][:, :], in_=src, func=AF.Copy, scale=ssq[:, b:b + 1]),
                 reads=[skey, f"ssq{b}"], writes=[f"xn{xb}"])

        def fe_b(g, b):
            xb = b % 2
            for u in range(4):
                bk = nbank()
                pb = banks[bk][:, :].bitcast(BF16)

                def tr(e, u=u, pb=pb, xb=xb):
                    ins = None
                    for j in range(4):
                        c = 4 * u + j
                        ins = e.transpose(out=pb[:, j * 128:(j + 1) * 128], in_=xn[xb][:, c * 128:(c + 1) * 128],
                                          identity=identb[:, :])
                    return ins
                S.op("pe", tr, reads=[f"xn{xb}", "identb"], writes=[f"bank{bk}"])
                S.op("dve", lambda e, u=u, pb=pb: e.tensor_tensor(
                    out=hT[:, 4 * u:4 * u + 4, b * 128:(b + 1) * 128],
                    in0=pb[:, 0:512].rearrange("p (j t) -> p j t", j=4),
                    in1=cv[:, G_N + 4 * u:G_N + 4 * u + 4].unsqueeze(2).to_broadcast([128, 4, 128]), op=ALU.mult),
                    reads=[f"bank{bk}", "cv"], writes=[f"hT{b}"])

        def fe_step(g, i):
            if i < 6:
                fe_a(g, i)
            if 1 <= i <= 6:
                fe_b(g, i - 1)

        HT_ALL = [f"hT{b}" for b in range(6)]
        AG_KEYS = [f"ag2_{h}" for h in range(8)]
        H2_KEYS = [f"h2_{c}" for c in range(8)]
        HT_OWN = [f"hT{b}" for b in range(1, 5)]

        def mm_unit(bk, ncols, wt, wkey, wcol0, tok0, nk=16, rhs_keys=HT_ALL):
            w3 = wt[:, :].rearrange("p (k c) -> p k c", k=16)

            def f(e):
                ins = None
                for k in range(nk):
                    ins = e.matmul(banks[bk][:, 0:ncols], lhsT=w3[:, k, wcol0:wcol0 + 128], rhs=hT[:, k, tok0:tok0 + ncols],
                                   start=(k == 0), stop=(k == nk - 1))
                return ins
            S.op("pe", f, reads=[wkey] + list(rhs_keys), writes=[f"bank{bk}"])

        def gate_evac(bk, ncols, out_ap, out_keys):
            ti = nth()
            S.op("act", lambda e: e.activation(out=th[ti][:, 0:ncols], in_=banks[bk][:, 0:ncols], func=AF.Tanh, scale=0.5),
                 reads=[f"bank{bk}"], writes=[f"th{ti}"])
            S.op("dve", lambda e: e.scalar_tensor_tensor(out=out_ap, in0=th[ti][:, 0:ncols], scalar=1.0, in1=banks[bk][:, 0:ncols],
                                                         op0=ALU.add, op1=ALU.mult),
                 reads=[f"th{ti}", f"bank{bk}"], writes=out_keys)

        def rope_evac(bk, ncols, pos0, out_ap, out_keys):
            ti = nth()
            S.op("act", lambda e: e.activation(out=th[ti][0:64, 0:ncols], in_=banks[bk][64:128, 0:ncols], func=AF.Copy),
                 reads=[f"bank{bk}"], writes=[f"th{ti}"])
            S.op("act", lambda e: e.activation(out=th[ti][64:128, 0:ncols], in_=banks[bk][0:64, 0:ncols], func=AF.Copy),
                 reads=[f"bank{bk}"], writes=[f"th{ti}"])
            t1 = ntt()
            S.op("dve", lambda e: e.tensor_tensor(out=tt[t1][:, 0:ncols], in0=banks[bk][:, 0:ncols], in1=cost[:, pos0:pos0 + ncols], op=ALU.mult),
                 reads=[f"bank{bk}", "cs"], writes=[f"tt{t1}"])
            S.op("dve", lambda e: e.tensor_tensor(out=th[ti][:, 0:ncols], in0=th[ti][:, 0:ncols], in1=sint[:, pos0:pos0 + ncols], op=ALU.mult),
                 reads=[f"th{ti}", "cs"], writes=[f"th{ti}"])
            S.op("dve", lambda e: e.tensor_tensor(out=out_ap, in0=tt[t1][:, 0:ncols], in1=th[ti][:, 0:ncols], op=ALU.add),
                 reads=[f"tt{t1}", f"th{ti}"], writes=out_keys)

        def group(g, last):
            S.dma("sp", cost[:, :], cs_d[0, :, 512 * g:512 * g + 768], writes=["cs"])
            S.dma("sp", sint[:, :], cs_d[1, :, 512 * g:512 * g + 768], writes=["cs"])

            if stage < 1:
                return
            st["avail"] = list(range(6))
            S1, S2 = 6, 7
            stats_fn = {}

            def conv_chunk(c):
                bk = nbank()
                half = NPE // 2
                for (t0_, t1_, key) in ((0, half, "dgA0"), (half, NPE, "dgA1")):
                    n_ = t1_ - t0_
                    S.op("pool", lambda e, t0_=t0_, n_=n_: e.tensor_tensor(
                        out=dgA[:, t0_:t0_ + n_, :], in0=identb[:, :].unsqueeze(1).to_broadcast([128, n_, 128]),
                        in1=cv[:, W_DW + 31 * c + t0_:W_DW + 31 * c + t0_ + n_].unsqueeze(2).to_broadcast([128, n_, 128]), op=ALU.mult),
                        reads=["identb", "cv"], writes=[key])

                    def fa(e, t0_=t0_, t1_=t1_):
                        ins = None
                        for tap in range(t0_, t1_):
                            ins = e.matmul(banks[bk][:, :], lhsT=dgA[:, tap, :], rhs=h2[:, c, tap + 1:tap + 513],
                                           start=(tap == 0), stop=(tap == NPE - 1))
                        return ins
                    S.op("pe", fa, reads=[key, f"h2_{c}"], writes=[f"bank{bk}"])
                accs = [(acc, "acc"), (tt[0], "tt0")]
                for n, tap in enumerate(range(NPE, 31)):
                    a_, ak = accs[n % 2]
                    wcol = cv[:, W_DW + 31 * c + tap:W_DW + 31 * c + tap + 1]
                    if n < 2:
                        S.op("dve", lambda e, a_=a_, tap=tap, wcol=wcol: e.tensor_scalar(
                            out=a_[:, :], in0=h2[:, c, tap + 1:tap + 513], scalar1=wcol, scalar2=None, op0=ALU.mult),
                            reads=[f"h2_{c}", "cv"], writes=[ak])
                    else:
                        S.op("dve", lambda e, a_=a_, tap=tap, wcol=wcol: e.scalar_tensor_tensor(
                            out=a_[:, :], in0=h2[:, c, tap + 1:tap + 513], scalar=wcol, in1=a_[:, :], op0=ALU.mult, op1=ALU.add),
                            reads=[f"h2_{c}", "cv", ak], writes=[ak])
                S.op("dve", lambda e: e.tensor_tensor(out=acc[:, :], in0=acc[:, :], in1=tt[0][:, :], op=ALU.add),
                     reads=["acc", "tt0"], writes=["acc"])
                S.op("dve", lambda e: e.tensor_tensor(out=tt[1][:, :], in0=banks[bk][:, :], in1=acc[:, :], op=ALU.add),
                     reads=[f"bank{bk}", "acc"], writes=["tt1"])
                si = c % 2
                S.op("act", lambda e: e.activation(out=hc[:, c, :], in_=tt[1][:, :], func=AF.Identity,
                                                   bias=cv[:, B_DW + c:B_DW + c + 1], scale=0.5),
                     reads=["tt1", "cv"], writes=[f"hc{c}"])
                S.op("act", lambda e: e.activation(out=sq[si][:, :], in_=tt[1][:, :], func=AF.Square,
                                                   bias=cv[:, B_DW + c:B_DW + c + 1], scale=0.5),
                     reads=["tt1", "cv"], writes=[f"sq{si}"])

                def fs(e):
                    e.matmul(banks[S1][:, :], lhsT=onesln[:, :], rhs=hc[:, c, :], start=(c == 0), stop=(c == 7))
                    return e.matmul(banks[S2][:, :], lhsT=onesln[:, :], rhs=sq[si][:, :], start=(c == 0), stop=(c == 7))
                stats_fn[c] = (fs, ["onesln", f"hc{c}", f"sq{si}"])
                if c > 0:
                    f_, rd_ = stats_fn[c - 1]
                    S.op("pe", f_, reads=rd_, writes=[f"bank{S1}", f"bank{S2}"])
                if c == 3 and g > 0:
                    for blk in range(4):
                        row = 512 * g + 128 + 128 * blk
                        S.dma("sp", r[:, blk, :], xs[row:row + 128, :], writes=[f"r{blk}"])

            for c in range(8):
                wt, wk = wslot()
                for hf in range(2):
                    tok0 = 112 + 272 * hf
                    bv = nbank()
                    mm_unit(bv, 272, wt, wk, 0, tok0)
                    bg = nbank()
                    mm_unit(bg, 272, wt, wk, 128, tok0)
                    ti = nth()
                    S.op("act", lambda e, bg=bg, ti=ti: e.activation(out=th[ti][:, 0:272], in_=banks[bg][:, 0:272], func=AF.Tanh, scale=0.5),
                         reads=[f"bank{bg}"], writes=[f"th{ti}"])
                    S.op("dve", lambda e, bv=bv, ti=ti, c=c, hf=hf: e.scalar_tensor_tensor(
                        out=h2[:, c, 272 * hf:272 * hf + 272], in0=th[ti][:, 0:272], scalar=1.0, in1=banks[bv][:, 0:272],
                        op0=ALU.add, op1=ALU.mult), reads=[f"th{ti}", f"bank{bv}", f"bank{bg}"], writes=[f"h2_{c}"] + AG_KEYS)
                if c > 0:
                    conv_chunk(c - 1)
            if stage < 1.3:
                return
            wt, wk = wslot()
            for kc in range(2):
                for hf in range(2):
                    bk = nbank()
                    mm_unit(bk, 384, wt, wk, 128 * kc, 384 * hf)
                    rope_evac(bk, 384, 384 * hf, kT[:, kc, 384 * hf:384 * hf + 384], [f"kT{kc}"])
            if stage < 1.6:
                return
            wt, wk = wslot()
            w3 = wt[:, :].rearrange("p (k c) -> p k c", k=16)
            for u in range(3):
                bk = nbank()

                def fv(e, u=u, bk=bk, w3=w3):
                    ins = None
                    for j in range(2):
                        b = 2 * u + j
                        for k in range(16):
                            ins = e.matmul(banks[bk][:, 256 * j:256 * j + 256], lhsT=hT[:, k, b * 128:(b + 1) * 128], rhs=w3[:, k, :],
                                           start=(k == 0), stop=(k == 15))
                    return ins
                S.op("pe", fv, reads=[wk] + HT_ALL, writes=[f"bank{bk}"])
                S.op("act", lambda e, u=u, bk=bk: e.activation(out=vv[:, 2 * u:2 * u + 2, :],
                                                               in_=banks[bk][:, :].rearrange("p (j c) -> p j c", j=2), func=AF.Copy),
                     reads=[f"bank{bk}"], writes=["vv"])

            conv_chunk(7)
            if stage < 2:
                return
            def ln_finalize():
              f_, rd_ = stats_fn[7]
              S.op("pe", f_, reads=rd_, writes=[f"bank{S1}", f"bank{S2}"])
              S.op("dve", lambda e: e.tensor_copy(out=lnm[:, :], in_=banks[S1][:, :]), reads=[f"bank{S1}"], writes=["lnm"])
              S.op("dve", lambda e: e.tensor_tensor(out=lnv[:, :], in0=banks[S1][:, :], in1=lnm[:, :], op=ALU.mult),
                   reads=[f"bank{S1}", "lnm"], writes=["osb", "onb0", "onb1"])
              S.op("dve", lambda e: e.tensor_tensor(out=lnv[:, :], in0=banks[S2][:, :], in1=lnv[:, :], op=ALU.subtract),
                   reads=[f"bank{S2}", "osb"], writes=["osb", "onb0", "onb1"])
              S.op("dve", lambda e: e.tensor_scalar(out=lnv[:, :], in0=lnv[:, :], scalar1=1e-5, scalar2=None, op0=ALU.add),
                   reads=["osb"], writes=["osb", "onb0", "onb1"])
              S.op("act", lambda e: e.activation(out=lnv[:, :], in_=lnv[:, :], func=AF.Ln), reads=["osb"], writes=["osb", "onb0", "onb1"])
              S.op("act", lambda e: e.activation(out=lnr[:, :], in_=lnv[:, :], func=AF.Exp, scale=-0.5), reads=["osb"], writes=["lnr"])
              S.op("dve", lambda e: e.scalar_tensor_tensor(out=lnm[:, :], in0=lnm[:, :], scalar=-1.0, in1=lnr[:, :], op0=ALU.mult, op1=ALU.mult),
                   reads=["lnm", "lnr"], writes=["lnm"])
            ln_tmp = {}
            ln_c = [0]

            def ln_s1(c):
                t1 = ntt()
                ti = nth()
                ln_tmp[c] = (t1, ti)
                S.op("dve", lambda e: e.tensor_tensor(out=tt[t1][:, :], in0=hc[:, c, :], in1=lnr[:, :], op=ALU.mult),
                     reads=[f"hc{c}", "lnr"], writes=[f"tt{t1}"])
                S.op("dve", lambda e: e.tensor_tensor(out=tt[t1][:, :], in0=tt[t1][:, :], in1=lnm[:, :], op=ALU.add),
                     reads=[f"tt{t1}", "lnm"], writes=[f"tt{t1}"])
                S.op("act", lambda e: e.activation(out=th[ti][:, :], in_=tt[t1][:, :], func=AF.Tanh,
                                                   bias=hl[:, 8 + c:9 + c], scale=hl[:, c:c + 1]),
                     reads=[f"tt{t1}", "hl"], writes=[f"th{ti}"])
                S.op("dve", lambda e: e.tensor_scalar(out=tt[t1][:, :], in0=tt[t1][:, :], scalar1=cv[:, LN_G + c:LN_G + c + 1],
                                                      scalar2=cv[:, LN_B + c:LN_B + c + 1], op0=ALU.mult, op1=ALU.add),
                     reads=[f"tt{t1}", "cv", f"th{ti}"], writes=[f"tt{t1}"])

            def ln_s2(c):
                t1, ti = ln_tmp[c]
                S.op("dve", lambda e: e.scalar_tensor_tensor(out=hc[:, c, :], in0=th[ti][:, :], scalar=1.0, in1=tt[t1][:, :],
                                                             op0=ALU.add, op1=ALU.mult),
                     reads=[f"th{ti}", f"tt{t1}"], writes=[f"hc{c}"])

            if stage < 3:
                return
            st["avail"] = list(range(6))
            HC_ALL = [f"hc{c}" for c in range(8)]
            pw_w = {}
            cg_w = {}

            def pw_unit(oc):
                wtp, wkp = pw_w[oc // 4]
                w8 = wtp[:, :].rearrange("p (k c) -> p k c", k=8)
                bk = nbank()

                def f(e):
                    ins = None
                    for k in range(8):
                        ins = e.matmul(banks[bk][:, :], lhsT=w8[:, k, (oc % 4) * 128:(oc % 4) * 128 + 128], rhs=hc[:, k, :],
                                       start=(k == 0), stop=(k == 7))
                    return ins
                S.op("pe", f, reads=[wkp] + HC_ALL, writes=[f"bank{bk}"])
                return bk

            for cgi in range(4):
                wt, wk = wslot()
                for j in range(2):
                    oc = 2 * cgi + j
                    bg = nbank()
                    mm_unit(bg, 512, wt, wk, 128 * j, 128, rhs_keys=HT_OWN)
                    S.op("act", lambda e, bg=bg: e.activation(out=den[:, :], in_=banks[bg][:, :], func=AF.Tanh, scale=0.5),
                         reads=[f"bank{bg}"], writes=["den"])
                    S.op("dve", lambda e, bg=bg, oc=oc: e.scalar_tensor_tensor(out=mixT[:, oc, :], in0=den[:, :], scalar=1.0, in1=banks[bg][:, :],
                                                                               op0=ALU.add, op1=ALU.mult),
                         reads=["den", f"bank{bg}"], writes=[f"mixT{oc}"])
                    if oc == 1:
                        ln_finalize()
                    if oc == 4:
                        st["avail"] = list(range(8))
                    if oc >= 2:
                        for _ in range(2):
                            if ln_c[0] < 8:
                                ln_s1(ln_c[0])
                                if ln_c[0] > 0:
                                    ln_s2(ln_c[0] - 1)
                                ln_c[0] += 1
            assert ln_c[0] == 8
            ln_s2(7)

            def q_head(wt, wk, j, h):
                bk = nbank()
                mm_unit(bk, 512, wt, wk, 128 * j, 128, rhs_keys=HT_OWN)
                rope_evac(bk, 512, 128, qT[:, h, :], [f"qT{h}"])

            def ag_head(wt, wk, j, h):
                bk = nbank()
                mm_unit(bk, 512, wt, wk, 128 * j, 128, rhs_keys=HT_OWN)
                gate_evac(bk, 512, ag2[:, h, :], [f"ag2_{h}"] + H2_KEYS)

            def pw_oc(oc):
                bp = pw_unit(oc)
                S.op("dve", lambda e, oc=oc, bp=bp: e.scalar_tensor_tensor(
                    out=mixT[:, oc, :], in0=banks[bp][:, :], scalar=0.25, in1=mixT[:, oc, :], op0=ALU.mult, op1=ALU.mult),
                    reads=[f"bank{bp}", f"mixT{oc}"], writes=[f"mixT{oc}"])

            if stage < 5:
                return
            scale = 1.0 / math.sqrt(128.0)
            nfe = 0
            units = [(qb, kvh) for kvh in range(2) for qb in range(4)]

            def att_scores(i):
                qb, kvh = units[i]
                Pi = Pb[i % 2]
                pk = f"P{i % 2}_"
                mprev = 2 if (g == 0 and qb == 0) else 0
                mnext = 3 if (last and qb == 3) else 1
                for j in range(3):
                    bk = nbank()
                    mi = {0: mprev, 1: None, 2: mnext}[j]

                    def fsc(e, j=j, qb=qb, kvh=kvh, bk=bk, mi=mi):
                        ins = e.matmul(banks[bk][:, :], lhsT=kT[:, kvh, (qb + j) * 128:(qb + j + 1) * 128],
                                       rhs=qT[:, 4 * kvh:4 * kvh + 4, qb * 128:(qb + 1) * 128], start=True, stop=(mi is None))
                        if mi is not None:
                            ins = e.matmul(banks[bk][:, :], lhsT=negm[:, :],
                                           rhs=maskb[:, mi, :].unsqueeze(1).to_broadcast([128, 4, 128]), start=False, stop=True)
                        return ins
                    S.op("pe", fsc, reads=[f"kT{kvh}", "negm", "maskb"] + [f"qT{4 * kvh + hh}" for hh in range(4)], writes=[f"bank{bk}"])
                    S.op("act", lambda e, j=j, bk=bk, Pi=Pi: e.activation(out=Pi[:, j, :], in_=banks[bk][:, :], func=AF.Exp, scale=scale),
                         reads=[f"bank{bk}"], writes=[pk + str(j)])

            onb = osb[:, :].bitcast(BF16)

            def att_rest_a(i):
                qb, kvh = units[i]
                Pi = Pb[i % 2]
                pkeys = [f"P{i % 2}_{j}" for j in range(3)]
                bo = nbank()

                def fo(e, bo=bo, qb=qb, kvh=kvh, Pi=Pi):
                    ins = None
                    for hh in range(4):
                        for j in range(3):
                            ins = e.matmul(banks[bo][:, hh * 128:(hh + 1) * 128], lhsT=Pi[:, j, hh * 128:(hh + 1) * 128],
                                           rhs=vv[:, qb + j, kvh * 128:(kvh + 1) * 128], start=(j == 0), stop=(j == 2))
                    return ins
                S.op("pe", fo, reads=["vv"] + pkeys, writes=[f"bank{bo}"])
                bd = nbank()

                def fden(e, bd=bd, Pi=Pi):
                    ins = None
                    for hh in range(4):
                        for j in range(3):
                            ins = e.matmul(banks[bd][:, hh:hh + 1], lhsT=Pi[:, j, hh * 128:(hh + 1) * 128], rhs=ones1[:, 0:1],
                                           start=(j == 0), stop=(j == 2))
                    return ins
                S.op("pe", fden, reads=["ones1"] + pkeys, writes=[f"bank{bd}"])
                S.op("dve", lambda e, bd=bd, kvh=kvh: e.tensor_tensor(out=rs[:, 0:4], in0=banks[bd][:, 0:4], in1=sinkexp[:, 4 * kvh:4 * kvh + 4], op=ALU.add),
                     reads=[f"bank{bd}", "sinkexp"], writes=["rs"])
                S.op("dve", lambda e: e.reciprocal(out=rs[:, 4:8], in_=rs[:, 0:4]), reads=["rs"], writes=["rr"])
                ob = i % 2
                S.op("dve", lambda e, bo=bo, ob=ob: e.tensor_tensor(
                    out=onb[:, 512 * ob:512 * ob + 512].rearrange("p (h d) -> p h d", h=4),
                    in0=banks[bo][:, :].rearrange("p (h d) -> p h d", h=4),
                    in1=rs[:, 4:8].unsqueeze(2).to_broadcast([128, 4, 128]), op=ALU.mult),
                    reads=[f"bank{bo}", "rr"], writes=[f"onb{ob}", "osb"])

            def att_rest_b(i):
                qb, kvh = units[i]
                ob = i % 2
                bt = nbank()
                pb = banks[bt][:, :].bitcast(BF16)

                def ft(e, pb=pb, ob=ob):
                    ins = None
                    for hh in range(4):
                        ins = e.transpose(out=pb[:, hh * 128:(hh + 1) * 128], in_=onb[:, 512 * ob + hh * 128:512 * ob + (hh + 1) * 128],
                                          identity=identb[:, :])
                    return ins
                S.op("pe", ft, reads=[f"onb{ob}", "identb"], writes=[f"bank{bt}"])
                S.op("dve", lambda e, qb=qb, kvh=kvh, pb=pb: e.scalar_tensor_tensor(
                    out=mixT[:, 8 + 4 * kvh:12 + 4 * kvh, qb * 128:(qb + 1) * 128],
                    in0=pb[:, 0:512].rearrange("p (h t) -> p h t", h=4), scalar=0.5,
                    in1=ag2[:, 4 * kvh:4 * kvh + 4, qb * 128:(qb + 1) * 128], op0=ALU.mult, op1=ALU.mult),
                    reads=[f"bank{bt}"] + [f"ag2_{4 * kvh + hh}" for hh in range(4)], writes=[f"mixT{8 + 4 * kvh + hh}" for hh in range(4)])

            def att_step(i):
                if i + 1 < 8:
                    att_scores(i + 1)
                att_rest_a(i)
                if i > 0:
                    att_rest_b(i - 1)
                if i == 7:
                    att_rest_b(7)

            for i in range(2):
                wt, wk = wslot()
                for j in range(2):
                    q_head(wt, wk, j, 2 * i + j)
            for i in range(2):
                wt, wk = wslot()
                for j in range(2):
                    ag_head(wt, wk, j, 2 * i + j)
            for i in range(2, 4):
                wt, wk = wslot()
                for j in range(2):
                    q_head(wt, wk, j, 2 * i + j)
                    n_ = 2 * (i - 2) + j
                    if n_ == 0:
                        att_scores(0)
                    else:
                        att_step(n_ - 1)
            for i in range(2, 4):
                wt, wk = wslot()
                for j in range(2):
                    ag_head(wt, wk, j, 2 * i + j)
                if i == 2:
                    att_step(3)
            for half in range(2):
                pw_w[half] = wslot()
                for jj in range(2):
                    pw_oc(4 * half + 2 * jj)
                    pw_oc(4 * half + 2 * jj + 1)
                    att_step(4 + 2 * half + jj)

            if stage < 6:
                return
            MIX_ALL = [f"mixT{k}" for k in range(16)]
            for cgi in range(8):
                wt, wk = wslot()
                w3o = wt[:, :].rearrange("p (k c) -> p k c", k=16)
                for pr in range(2):
                    bk = nbank()

                    def fw(e, pr=pr, bk=bk, w3o=w3o):
                        ins = None
                        for j in range(2):
                            blk = 2 * pr + j
                            for k in range(16):
                                ins = e.matmul(banks[bk][:, 256 * j:256 * j + 256], lhsT=mixT[:, k, blk * 128:(blk + 1) * 128], rhs=w3o[:, k, :],
                                               start=(k == 0), stop=(k == 15))
                        return ins
                    S.op("pe", fw, reads=[wk] + MIX_ALL, writes=[f"bank{bk}"])
                    S.op("dve", lambda e, pr=pr, bk=bk, cgi=cgi: e.tensor_tensor(
                        out=r[:, 2 * pr:2 * pr + 2, cgi * 256:(cgi + 1) * 256], in0=r[:, 2 * pr:2 * pr + 2, cgi * 256:(cgi + 1) * 256],
                        in1=banks[bk][:, :].rearrange("p (j c) -> p j c", j=2), op=ALU.add),
                        reads=[f"bank{bk}", f"r{2 * pr}", f"r{2 * pr + 1}"], writes=[f"r{2 * pr}", f"r{2 * pr + 1}"])
                    if (not last) and nfe < 7 and (2 * cgi + pr) % 2 == 0:
                        fe_step(g + 1, nfe)
                        nfe += 1
            for blk in range(4):
                S.op("act", lambda e, blk=blk: e.activation(out=junk[:, 0:4, :], in_=r[:, blk, :].rearrange("p (a b) -> p a b", a=4), func=AF.Square, accum_out=ssq2[:, blk:blk + 1]),
                     reads=[f"r{blk}"], writes=[f"hc{c_}" for c_ in range(4)] + [f"ssq2_{blk}"])
                S.op("dve", lambda e, blk=blk: e.tensor_scalar(out=ssq2[:, blk:blk + 1], in0=ssq2[:, blk:blk + 1], scalar1=1.0 / D, scalar2=1e-6,
                                                               op0=ALU.mult, op1=ALU.add), reads=[f"ssq2_{blk}"], writes=[f"ssq2_{blk}"])
                S.op("pool", lambda e, blk=blk: e.tensor_tensor(out=ssq2[:, blk:blk + 1], in0=ssq2[:, blk:blk + 1], in1=mhalf[:, 0:1], op=ALU.pow),
                     reads=[f"ssq2_{blk}", "mhalf"], writes=[f"ssq2_{blk}"])
                S.op("dve", lambda e, blk=blk: e.scalar_tensor_tensor(out=r[:, blk, :], in0=r[:, blk, :], scalar=ssq2[:, blk:blk + 1], in1=gfin[:, :],
                                                                      op0=ALU.mult, op1=ALU.mult),
                     reads=[f"r{blk}", f"ssq2_{blk}", "gfin"], writes=[f"r{blk}"])
                row = 512 * g + 128 * blk
                S.dma("sp", y[row:row + 128, :], r[:, blk, :], reads=[f"r{blk}"])

        for i in range(7):
            fe_step(0, i)
        for g in range(ngroups):
            group(g, g == ngroups - 1)
        S.finish("sp")
        S.emit(block)
    return nc


def _slot(m):
    K, C = m.shape
    nk = K // 128
    return np.ascontiguousarray(m.reshape(nk, 128, C).transpose(1, 0, 2)).reshape(128, nk * C)


def _build_slots(w_in, w_pw, w_out):
    sl = np.empty((NSLOT, 128, 4096), np.float32)
    V, GL, CG, Q, K, VV, AG = 0, 1024, 2048, 3072, 4096, 4352, 4608
    for c in range(8):
        sl[c] = _slot(np.concatenate([w_in[:, V + c * 128:V + (c + 1) * 128], w_in[:, GL + c * 128:GL + (c + 1) * 128]], axis=1))
    sl[8] = _slot(w_in[:, K:K + 256])
    sl[9] = _slot(w_in[:, VV:VV + 256])
    for i in range(4):
        sl[10 + i] = _slot(w_in[:, CG + 256 * i:CG + 256 * (i + 1)])
    for i in range(2):
        sl[14 + i] = _slot(w_in[:, Q + 256 * i:Q + 256 * (i + 1)])
        sl[16 + i] = _slot(w_in[:, AG + 256 * i:AG + 256 * (i + 1)])
        sl[18 + i] = _slot(w_in[:, Q + 256 * (i + 2):Q + 256 * (i + 3)])
        sl[20 + i] = _slot(w_in[:, AG + 256 * (i + 2):AG + 256 * (i + 3)])
    sl[22] = _slot(w_pw[:, 0:512])
    sl[23] = _slot(w_pw[:, 512:1024])
    for i in range(8):
        sl[24 + i] = _slot(w_out[:, 256 * i:256 * (i + 1)])
    return sl


def _rope_tables(pos):
    half = 64
    inv = (1.0 / (10000.0 ** (np.arange(half, dtype=np.float32) / half))).astype(np.float32)
    ang = pos.astype(np.float32)[None, :] * inv[:, None]
    c = np.cos(ang).astype(np.float32)
    s = np.sin(ang).astype(np.float32)
    cos = np.concatenate([c, c], axis=0)
    sin = np.concatenate([-s, s], axis=0)
    return np.stack([cos, sin], axis=0).astype(np.float32)


_NC_CACHE = {}


def kernel(x_prompt, x_sample, norm_g, w_in, w_dw, b_dw, conv_ln_g, conv_ln_b, w_pw, attn_sink, w_out, final_norm_g):
    x_prompt = np.asarray(x_prompt, np.float32)
    x_sample = np.asarray(x_sample, np.float32)
    w_in0 = np.asarray(w_in, np.float32)[0]
    w_pw0 = np.asarray(w_pw, np.float32)[0]
    w_out0 = np.asarray(w_out, np.float32)[0]
    slots = _build_slots(w_in0, w_pw0, w_out0)

    cvec = np.zeros((128, NV), np.float32)
    cvec[:, 0:16] = np.asarray(norm_g, np.float32)[0].reshape(16, 128).T
    cvec[:, 16:24] = np.asarray(b_dw, np.float32)[0].reshape(8, 128).T
    cvec[:, 24:32] = np.asarray(conv_ln_g, np.float32)[0].reshape(8, 128).T
    cvec[:, 32:40] = np.asarray(conv_ln_b, np.float32)[0].reshape(8, 128).T
    cvec[:, 40:48] = np.broadcast_to(np.asarray(attn_sink, np.float32)[0][None, :], (128, 8))
    wd = np.asarray(w_dw, np.float32)[0]
    cvec[:, 48:] = wd.reshape(31, 8, 128).transpose(2, 1, 0).reshape(128, 248)
    gfin = np.ascontiguousarray(np.broadcast_to(np.asarray(final_norm_g, np.float32)[None, :], (128, D)))
    ident = np.eye(128, dtype=np.float32)
    jj = np.arange(128)[:, None]
    qq = np.arange(128)[None, :]
    m_prev = (jj >= qq).astype(np.float32)
    m_next = (jj <= qq).astype(np.float32)

    in_maps = []
    for c in range(NCORES):
        if c < 4:
            seq = x_prompt[c]
            start, vl, vr = 0, False, False
        else:
            seq = x_sample[(c - 4) // 2]
            half = (c - 4) % 2
            start, vl, vr = half * OWN, half == 1, half == 0
        xs = np.zeros((STREAM, D), np.float32)
        xs[128:128 + OWN] = seq[start:start + OWN]
        if vl:
            xs[0:128] = seq[start - 128:start]
        if vr:
            xs[128 + OWN:] = seq[start + OWN:start + OWN + 128]
        masks = np.concatenate([m_prev, m_next, m_prev if vl else np.zeros_like(m_prev),
                                m_next if vr else np.zeros_like(m_next)], axis=1)
        pos = np.arange(STREAM, dtype=np.float32) + (start - 128)
        in_maps.append({"xs": xs, "wsl": slots, "cvec": cvec, "gfin": gfin, "ident": ident,
                        "masks": np.ascontiguousarray(masks), "cs": _rope_tables(pos)})

    if "nc" not in _NC_CACHE:
        _NC_CACHE["nc"] = build_nc()
    res = run_bass_kernel_spmd(_NC_CACHE["nc"], in_maps, core_ids=list(range(NCORES)))
    outs = [np.asarray(r_["y"], np.float32) for r_ in res.results]
    y_prompt = np.stack(outs[0:4], axis=0)
    y_sample = np.stack([np.concatenate([outs[4], outs[5]], axis=0), np.concatenate([outs[6], outs[7]], axis=0)], axis=0)
    return (y_prompt, y_sample)
```
